# Optimizing a Trainium2 kernel written in Bass

```python
import math
import jax, jax.numpy as jnp
from jax import lax
import numpy as np

D_MODEL = 2048
BATCH = 1
SEQ = 8192
DEPTH = 1

CHUNK = 64
Q_BLOCK = 128
CONV_WIDTH = D_MODEL // 2
ATTN_WIDTH = D_MODEL - CONV_WIDTH
CONV_KERNEL = 31
N_HEADS = 8
V_HEAD_DIM = ATTN_WIDTH // N_HEADS
QK_HEAD_DIM = V_HEAD_DIM // 2
D_FF = 5632
N_MOD = 9
IN_WIDTH = 2 * CONV_WIDTH + 3 * ATTN_WIDTH
EPS = 1e-6
NEG_INF = -1e30

kernel_name = 'hybrid_conformer_diffattn_macaron_block'


def rms_norm(x, g):
    xf = x.astype(jnp.float32)
    y = xf * lax.rsqrt(jnp.mean(xf * xf, axis=-1, keepdims=True) + EPS)
    return (y * g.astype(jnp.float32)).astype(x.dtype)


def layer_norm(x, g, b):
    xf = x.astype(jnp.float32)
    mu = jnp.mean(xf, axis=-1, keepdims=True)
    xc = xf - mu
    y = xc * lax.rsqrt(jnp.mean(xc * xc, axis=-1, keepdims=True) + EPS)
    return (y * g.astype(jnp.float32) + b.astype(jnp.float32)).astype(x.dtype)


def modulate(h, shift, scale):
    return h * (1 + scale) + shift


def swiglu(h, w_gate, w_up, w_down):
    return (jax.nn.silu(h @ w_gate) * (h @ w_up)) @ w_down


def alibi_slopes():
    return 2.0 ** (-8.0 * (jnp.arange(N_HEADS, dtype=jnp.float32) + 1.0) / N_HEADS)


def lambda_init_fn(layer_idx):
    return 0.8 - 0.6 * math.exp(-0.3 * layer_idx)


def conformer_conv(u, b_in, w_dw, b_dw, ln_g, ln_b):
    u = u + b_in
    a, g = jnp.split(u, 2, axis=-1)
    v = a * jax.nn.sigmoid(g)
    v = lax.conv_general_dilated(
        v, w_dw[:, None, :], window_strides=(1,),
        padding=[(CONV_KERNEL - 1, 0)],
        dimension_numbers=('NWC', 'WIO', 'NWC'),
        feature_group_count=CONV_WIDTH) + b_dw
    return jax.nn.silu(layer_norm(v, ln_g, ln_b))


def diff_attention(q, k, v, lam, subln_g, lam_init):
    B, S = q.shape[0], q.shape[1]
    n_blk = S // Q_BLOCK
    scale = QK_HEAD_DIM ** -0.5
    slopes = alibi_slopes()
    k_pos = jnp.arange(S)
    k_chunk = k_pos // CHUNK
    q_blocks = q.reshape(B, n_blk, Q_BLOCK, N_HEADS, 2, QK_HEAD_DIM).transpose(1, 0, 2, 3, 4, 5)

    def block(args):
        q_blk, i = args
        q_pos = i * Q_BLOCK + jnp.arange(Q_BLOCK)
        s = jnp.einsum('bqhmd,bkhmd->bhmqk', q_blk, k).astype(jnp.float32) * scale
        dist = jnp.abs(q_pos[:, None] - k_pos[None, :]).astype(jnp.float32)
        allowed = k_chunk[None, :] <= (q_pos // CHUNK)[:, None]
        s = jnp.where(allowed, s - slopes[None, :, None, None, None] * dist, NEG_INF)
        p = jax.nn.softmax(s, axis=-1)
        a = p[:, :, 0] - lam * p[:, :, 1]
        return jnp.einsum('bhqk,bkhe->bqhe', a.astype(v.dtype), v)

    o = lax.map(block, (q_blocks, jnp.arange(n_blk)))
    o = o.transpose(1, 0, 2, 3, 4).reshape(B, S, N_HEADS, V_HEAD_DIM)
    o = rms_norm(o, subln_g) * (1.0 - lam_init)
    return o.reshape(B, S, ATTN_WIDTH)


def setup_inputs(seed: int = 0) -> dict:
    key = jax.random.key(seed)
    ks = jax.random.split(key, 26)
    f32 = jnp.float32
    D, L = D_MODEL, DEPTH

    def nrm(k, shape, std):
        return jax.random.normal(k, shape, f32) * std

    return {
        'x': nrm(ks[0], (BATCH, SEQ, D), 1.0),
        'c': nrm(ks[1], (BATCH, D), 1.0),
        'w_ada': nrm(ks[2], (L, D, N_MOD * D), 0.5 * D ** -0.5),
        'b_ada': nrm(ks[3], (L, N_MOD * D), 0.01),
        'g_pre': 1.0 + nrm(ks[4], (L, 3, D), 0.02),
        'g_post': 1.0 + nrm(ks[5], (L, 3, D), 0.02),
        'w_ffn1_gate': nrm(ks[6], (L, D, D_FF), D ** -0.5),
        'w_ffn1_up': nrm(ks[7], (L, D, D_FF), D ** -0.5),
        'w_ffn1_down': nrm(ks[8], (L, D_FF, D), D_FF ** -0.5),
        'w_in': nrm(ks[9], (L, D, IN_WIDTH), D ** -0.5),
        'b_in_conv': nrm(ks[10], (L, 2 * CONV_WIDTH), 0.02),
        'w_dw': nrm(ks[11], (L, CONV_KERNEL, CONV_WIDTH), CONV_KERNEL ** -0.5),
        'b_dw': nrm(ks[12], (L, CONV_WIDTH), 0.02),
        'conv_ln_g': 1.0 + nrm(ks[13], (L, CONV_WIDTH), 0.02),
        'conv_ln_b': nrm(ks[14], (L, CONV_WIDTH), 0.02),
        'lam_q1': nrm(ks[15], (L, QK_HEAD_DIM), 0.1),
        'lam_k1': nrm(ks[16], (L, QK_HEAD_DIM), 0.1),
        'lam_q2': nrm(ks[17], (L, QK_HEAD_DIM), 0.1),
        'lam_k2': nrm(ks[18], (L, QK_HEAD_DIM), 0.1),
        'subln_g': 1.0 + nrm(ks[19], (L, V_HEAD_DIM), 0.02),
        'w_out': nrm(ks[20], (L, CONV_WIDTH + ATTN_WIDTH, D), (CONV_WIDTH + ATTN_WIDTH) ** -0.5),
        'w_ffn2_gate': nrm(ks[21], (L, D, D_FF), D ** -0.5),
        'w_ffn2_up': nrm(ks[22], (L, D, D_FF), D ** -0.5),
        'w_ffn2_down': nrm(ks[23], (L, D_FF, D), D_FF ** -0.5),
    }


def reference(x, c, w_ada, b_ada, g_pre, g_post, w_ffn1_gate, w_ffn1_up, w_ffn1_down,
              w_in, b_in_conv, w_dw, b_dw, conv_ln_g, conv_ln_b,
              lam_q1, lam_k1, lam_q2, lam_k2, subln_g, w_out,
              w_ffn2_gate, w_ffn2_up, w_ffn2_down):
    B, S = x.shape[0], x.shape[1]
    split_pts = [2 * CONV_WIDTH, 2 * CONV_WIDTH + ATTN_WIDTH, 2 * CONV_WIDTH + 2 * ATTN_WIDTH]
    for l in range(DEPTH):
        lam_init = lambda_init_fn(l)
        mod = (jax.nn.silu(c) @ w_ada[l] + b_ada[l])[:, None, :]
        sh1, sc1, gt1, sh2, sc2, gt2, sh3, sc3, gt3 = jnp.split(mod, N_MOD, axis=-1)

        h = modulate(rms_norm(x, g_pre[l, 0]), sh1, sc1)
        x = x + 0.5 * gt1 * rms_norm(swiglu(h, w_ffn1_gate[l], w_ffn1_up[l], w_ffn1_down[l]), g_post[l, 0])

        h = modulate(rms_norm(x, g_pre[l, 1]), sh2, sc2)
        proj = h @ w_in[l]
        u_conv, q, k, v = jnp.split(proj, split_pts, axis=-1)
        y_conv = conformer_conv(u_conv, b_in_conv[l], w_dw[l], b_dw[l], conv_ln_g[l], conv_ln_b[l])
        q = q.reshape(B, S, N_HEADS, 2, QK_HEAD_DIM)
        k = k.reshape(B, S, N_HEADS, 2, QK_HEAD_DIM)
        v = v.reshape(B, S, N_HEADS, V_HEAD_DIM)
        lam = (jnp.exp(jnp.sum(lam_q1[l].astype(jnp.float32) * lam_k1[l].astype(jnp.float32)))
               - jnp.exp(jnp.sum(lam_q2[l].astype(jnp.float32) * lam_k2[l].astype(jnp.float32)))
               + lam_init)
        y_attn = diff_attention(q, k, v, lam, subln_g[l], lam_init)
        y_mix = jnp.concatenate([y_conv, y_attn], axis=-1) @ w_out[l]
        x = x + gt2 * rms_norm(y_mix, g_post[l, 1])

        h = modulate(rms_norm(x, g_pre[l, 2]), sh3, sc3)
        x = x + 0.5 * gt3 * rms_norm(swiglu(h, w_ffn2_gate[l], w_ffn2_up[l], w_ffn2_down[l]), g_post[l, 2])
    return x
```

```python
import numpy as np
import ml_dtypes
import concourse.bass as bass
import concourse.mybir as mybir
from concourse.bass_utils import run_bass_kernel_spmd
from contextlib import ExitStack

F32, BF16 = mybir.dt.float32, mybir.dt.bfloat16
AF = mybir.ActivationFunctionType
ALU = mybir.AluOpType
AX = mybir.AxisListType
D = 2048; DFF = 5632; NH = 8; KC = 16; FC = 44; EPS = 1e-6; NCORE = 8
BIG = 1.0e6


class Tok:
    __slots__ = ("sem", "val", "key")

    def __init__(self, sem, val, key):
        self.sem = sem; self.val = val; self.key = key


class Eng:
    def __init__(self, nc, es, eng, name, selfwait=True):
        self.e = eng; self.name = name; self.cnt = 0; self.seen = {}
        self.sem = es.enter_context(nc.semaphore("s_" + name))
        self.selfwait = selfwait

    def wait(self, deps):
        for t in deps:
            if t is None:
                continue
            if isinstance(t, (list, tuple)):
                self.wait(t); continue
            if t.key == self.name and not self.selfwait:
                continue
            if self.seen.get(t.key, 0) < t.val:
                self.e.wait_ge(t.sem, t.val); self.seen[t.key] = t.val

    def op(self, deps, inst_fn):
        self.wait(deps)
        inst = inst_fn(self.e)
        self.cnt += 1
        inst.then_inc(self.sem, 1)
        return Tok(self.sem, self.cnt, self.name)

    def last(self):
        return Tok(self.sem, self.cnt, self.name) if self.cnt else None


class DSem:
    def __init__(self, nc, es, name):
        self.sem = es.enter_context(nc.semaphore("d_" + name)); self.name = "d_" + name; self.n = 0

    def issue(self, q, deps, out, in_, **kw):
        q.wait(deps)
        q.e.dma_start(out=out, in_=in_, **kw).then_inc(self.sem, 16)
        self.n += 1
        return Tok(self.sem, 16 * self.n, self.name)

    def last(self):
        return Tok(self.sem, 16 * self.n, self.name) if self.n else None


def build(NP, STOP=None):
    import os
    STOP = STOP or os.environ.get('KSTOP')
    NT = 256 * NP; TG = min(512, NT); NG = NT // TG; NB = NT // 128
    NE = sum(16 * p + 14 for p in range(NP))
    nc = bass.Bass("TRN2", target_bir_lowering=False)

    def din(name, shape, dt=F32):
        return nc.dram_tensor(name, shape, dt, kind="ExternalInput").ap()

    x_sh = din("x_sh", [NT, D]); c_in = din("c", [1, D])
    wada = din("wada", [D, 2304]); bada = din("bada", [1, 2304])
    g_pre = din("g_pre", [48, 128]); g_post = din("g_post", [48, 128])
    w1g = din("w1g", [D, DFF]); w1u = din("w1u", [D, DFF]); w1d = din("w1d", [DFF, D])
    w3g = din("w3g", [D, DFF]); w3u = din("w3u", [D, DFF]); w3d = din("w3d", [DFF, D])
    w_in = din("w_in", [D, 5120]); w_out = din("w_out", [D, D])
    b_inc = din("b_inc", [16, 128]); w_dw = din("w_dw", [31, 1024]); b_dw = din("b_dw", [8, 128])
    ln_g = din("ln_g", [8, 128]); ln_b = din("ln_b", [8, 128])
    lamv = din("lamv", [4, 64]); subg = din("subg", [1, 128])
    ident_d = din("ident", [128, 128]); identb_d = din("identb", [128, 128], BF16)
    kaug_d = din("kaug", [2, 2, 2048], BF16); qaug_d = din("qaug", [2, 16, NT], BF16)
    dtab_d = din("dtab", [8, 128, 2, 512], BF16)
    ctab_d = din("ctab", [128, 16 + NE])
    out_sh = nc.dram_tensor("out_sh", [NT, D], F32, kind="ExternalOutput").ap()
    mod_in = nc.dram_tensor("mod_in", [1, 2304], F32); mod_all = nc.dram_tensor("mod_all", [8, 2304], F32)
    xs = nc.dram_tensor("xs", [16, 128, NT], F32).ap()
    tails_in = nc.dram_tensor("tails_in", [1024, NP * 32], F32)
    tails_all = nc.dram_tensor("tails_all", [8 * 1024, NP * 32], F32)
    k_in = [nc.dram_tensor("k_in%d" % i, [256, NT], BF16) for i in range(4)]
    k_all = [nc.dram_tensor("k_all%d" % i, [8 * 256, NT], BF16) for i in range(4)]
    v_in = [nc.dram_tensor("v_in%d" % i, [2 * NT, 128], BF16) for i in range(4)]
    v_all = [nc.dram_tensor("v_all%d" % i, [16 * NT, 128], BF16) for i in range(4)]

    es = ExitStack()
    ar = es.enter_context(nc.sbuf_tensor("arena", [128, 53200], F32))
    ps = es.enter_context(nc.psum_tensor("ps", [128, 8, 512], F32))
    A0 = 0; Y0 = 22528; H0 = 38912; W0 = 47104; P0 = 52736

    def F(off, n):
        return ar[:, off:off + n]

    def B(off, nw):
        return ar[:, off:off + nw].bitcast(BF16)

    pe = Eng(nc, es, nc.tensor, "pe", selfwait=False)
    act = Eng(nc, es, nc.scalar, "act")
    dve = Eng(nc, es, nc.vector, "dve")
    gp = Eng(nc, es, nc.gpsimd, "gp")
    sp = Eng(nc, es, nc.sync, "sp")
    dsems = []

    def mk(name):
        d = DSem(nc, es, name); dsems.append(d); return d

    def barrier():
        toks = [e.last() for e in (pe, act, dve)] + [d.last() for d in dsems]
        for e in (pe, act, dve, sp, gp):
            e.wait(toks)

    vec = F(P0, 280)
    nlam = F(P0 + 280, 1)
    Yv = F(Y0, 16 * NT).rearrange("p (k t) -> p k t", t=NT)
    Hv = B(H0, 8 * NT).rearrange("p (k t) -> p k t", t=NT)
    ATv = B(A0, 22 * NT).rearrange("p (k t) -> p k t", t=NT)

    wslots = [B(W0 + 1408 * i, 1408) for i in range(4)]
    wsem = [mk("w%d" % i) for i in range(4)]
    wfree = [[] for _ in range(4)]
    plan = []

    def colt(w, kcs, k0, n0):
        return w.rearrange("(k p) n -> p k n", p=128)[:, k0:k0 + kcs, n0:n0 + 128]

    for (wg, wu) in ((w1g, w1u),):
        for m in range(FC):
            plan.append((colt(wg, 16, 0, m * 128), 16)); plan.append((colt(wu, 16, 0, m * 128), 16))
    for dc in range(16):
        for kh in range(2):
            plan.append((colt(w1d, 22, kh * 22, dc * 128), 22))
    for j in range(8):
        plan.append((colt(w_in, 16, 0, j * 128), 16)); plan.append((colt(w_in, 16, 0, 1024 + j * 128), 16))
    for h in range(8):
        for o in (2048, 3072, 4096):
            plan.append((colt(w_in, 16, 0, o + h * 128), 16))
    for dc in range(16):
        plan.append((colt(w_out, 16, 0, dc * 128), 16))
    for m in range(FC):
        plan.append((colt(w3g, 16, 0, m * 128), 16)); plan.append((colt(w3u, 16, 0, m * 128), 16))
    for dc in range(16):
        for kh in range(2):
            plan.append((colt(w3d, 22, kh * 22, dc * 128), 22))
    wst = {"issued": 0, "consumed": 0, "released": 0}

    def w_issue():
        i = wst["issued"]; s = i % 4
        src, kn = plan[i]
        dst = wslots[s][:, 0:kn * 128].rearrange("p (k n) -> p k n", n=128)
        tok = wsem[s].issue(gp, wfree[s], dst, src)
        wst["issued"] += 1
        return tok

    wtoks = {}

    def w_fetch():
        i = wst["consumed"]
        while wst["issued"] < len(plan) and wst["issued"] <= i + 3 and wst["issued"] - 4 < wst["released"]:
            idx_ = wst["issued"]; wtoks[idx_] = w_issue()
        assert i in wtoks, (i, wst)
        wst["consumed"] += 1
        kn = plan[i][1]
        return i, wslots[i % 4][:, 0:kn * 128].rearrange("p (k n) -> p k n", n=128), wtoks.pop(i)

    def w_release(i, toks):
        wfree[i % 4] = list(toks); wst["released"] = i + 1

    cst = mk("cst"); sd = mk("sd")
    identF = F(A0, 128); rows1 = F(A0 + 128, 128); rows2 = F(A0 + 256, 128); rows3 = F(A0 + 384, 128)
    cT = F(A0 + 512, 16); scT = F(A0 + 528, 16)
    lamt = F(A0 + 544, 256); lamr = F(A0 + 800, 8)
    aw = [F(A0 + 1024 + i * 2304, 2304) for i in range(3)]
    modrow = F(A0 + 1024 + 3 * 2304, 2304); brow = F(A0 + 1024 + 4 * 2304, 2304)
    xblk = [F(A0 + 13000 + i * 2048, 2048) for i in range(2)]
    t0 = []
    t0.append(cst.issue(sp, [], identF, ident_d))
    t0.append(cst.issue(sp, [], cT, c_in.rearrange("o (k p) -> p (o k)", p=128), allow_slow_non_contiguous=True))
    t0.append(cst.issue(sp, [], brow[0:1, :], bada))
    t0.append(cst.issue(sp, [], rows2[16:64, :], g_pre))
    t0.append(cst.issue(sp, [], rows2[64:112, :], g_post))
    t0.append(cst.issue(sp, [], rows3[0:16, :], b_inc))
    t0.append(cst.issue(sp, [], rows3[16:24, :], b_dw))
    t0.append(cst.issue(sp, [], rows3[24:32, :], ln_g))
    t0.append(cst.issue(sp, [], rows3[32:40, :], ln_b))
    for i in range(4):
        t0.append(cst.issue(sp, [], lamt[:, i * 64:(i + 1) * 64], lamv[i:i + 1, :].broadcast_to([128, 64])))
    c_ready = cst.last()
    t_sc = act.op([c_ready], lambda e: e.activation(out=scT, in_=cT, func=AF.Silu))
    awsem = [mk("aw%d" % i) for i in range(3)]
    awfree = [[] for _ in range(3)]
    wav = wada.rearrange("(k p) n -> p k n", p=128)
    NS5 = [(0, 512), (512, 512), (1024, 512), (1536, 512), (2048, 256)]
    xsem = [mk("xb0"), mk("xb1")]; xfree = [[], []]
    bfree = {}
    evs = [0, 0]

    def ada_kc(kc):
        s_ = kc % 3
        lt_ = awsem[s_].issue(gp, awfree[s_], aw[s_], wav[:, kc, :])
        last_ = None
        for n, (n0, w) in enumerate(NS5):
            last_ = pe.op([lt_, t_sc], lambda e, n=n, n0=n0, w=w: e.matmul(
                ps[0:1, n, 0:w], lhsT=scT[:, kc:kc + 1], rhs=aw[s_][:, n0:n0 + w], start=(kc == 0), stop=(kc == KC - 1)))
        awfree[s_] = [last_]

    def x_tb(tb):
        s_ = tb % 2
        lt_ = xsem[s_].issue(sp, xfree[s_], xblk[s_], x_sh[tb * 128:(tb + 1) * 128, :])
        lastpe = None
        for g4 in range(4):
            bank = 5 + (evs[1] % 3); evs[1] += 1
            for j in range(4):
                kc = g4 * 4 + j
                lastpe = pe.op([lt_, c_ready, bfree.get(bank)], lambda e, bank=bank, j=j, kc=kc: e.transpose(
                    out=ps[:, bank, j * 128:(j + 1) * 128], in_=xblk[s_][:, kc * 128:(kc + 1) * 128], identity=identF))
            src = ps[:, bank, :].rearrange("p (a b) -> p a b", b=128)
            dst = Yv[:, g4 * 4:g4 * 4 + 4, tb * 128:(tb + 1) * 128]
            if evs[0] % 2 == 0:
                bfree[bank] = act.op([lastpe], lambda e, src=src, dst=dst: e.activation(out=dst, in_=src, func=AF.Copy))
            else:
                bfree[bank] = dve.op([lastpe], lambda e, src=src, dst=dst: e.tensor_copy(out=dst, in_=src))
            evs[0] += 1
        xfree[s_] = [lastpe]

    kcs_done = 0
    for tb in range(NB):
        x_tb(tb)
        n_k = (KC * (tb + 1)) // NB - kcs_done
        for _ in range(n_k):
            ada_kc(kcs_done); kcs_done += 1
    while kcs_done < KC:
        ada_kc(kcs_done); kcs_done += 1
    xo = mk("xo")
    xo.issue(sp, [act.last(), dve.last()], xs.rearrange("k p t -> p k t"), Yv)
    psflat = ps[0:1, 0:5, :].rearrange("p a b -> p (a b)")
    t_mr = dve.op([pe.last(), c_ready], lambda e: e.tensor_tensor(out=modrow[0:1, :], in0=psflat[:, 0:2304], in1=brow[0:1, :], op=ALU.add))
    t_mo = sd.issue(gp, [t_mr], mod_in[:, :], modrow[0:1, :])
    ccn = [0]

    def allgather(deps, src, dst):
        gp.wait(deps)
        ccn[0] += 1
        sem_ = es.enter_context(nc.semaphore("cc%d" % ccn[0]))
        nc.gpsimd.collective_compute("AllGather", ALU.bypass, replica_groups=[list(range(NCORE))],
                                     ins=[src.ap().opt()], outs=[dst.ap().opt()]).then_inc(sem_)
        return Tok(sem_, 1, "cc%d" % ccn[0])

    t_ag = allgather([t_mo], mod_in, mod_all)
    mav = mod_all.ap().rearrange("r (a b) -> (r a) b", b=128)
    t1 = cst.issue(sp, [t_ag], rows1, mav[0:128, :])
    t2 = cst.issue(sp, [t_ag], rows2[0:16, :], mav[128:144, :])
    tr = [t2, bfree.get(5), bfree.get(6), bfree.get(7)]
    pe.op(tr, lambda e: e.transpose(out=ps[:, 5, 0:128], in_=rows1, identity=identF))
    pe.op(tr, lambda e: e.transpose(out=ps[:, 6, 0:112], in_=rows2[0:112, :], identity=identF[0:112, 0:112]))
    t_tr = pe.op(tr, lambda e: e.transpose(out=ps[:, 7, 0:40], in_=rows3[0:40, :], identity=identF[0:40, 0:40]))
    dve.op([t_tr], lambda e: e.tensor_copy(out=vec[:, 0:128], in_=ps[:, 5, 0:128]))
    dve.op([t_tr], lambda e: e.tensor_copy(out=vec[:, 128:240], in_=ps[:, 6, 0:112]))
    tv = dve.op([t_tr], lambda e: e.tensor_copy(out=vec[:, 240:280], in_=ps[:, 7, 0:40]))
    for s, coef in enumerate((0.5, 1.0, 0.5)):
        sc = vec[:, (3 * s + 1) * 16:(3 * s + 2) * 16]; gt = vec[:, (3 * s + 2) * 16:(3 * s + 3) * 16]
        gp_ = vec[:, 144 + 16 * s:160 + 16 * s]; go = vec[:, 192 + 16 * s:208 + 16 * s]
        dve.op([tv], lambda e, sc=sc, gp_=gp_: e.scalar_tensor_tensor(out=sc, in0=sc, scalar=1.0, in1=gp_, op0=ALU.add, op1=ALU.mult))
        tv = dve.op([tv], lambda e, gt=gt, go=go, coef=coef: e.scalar_tensor_tensor(out=gt, in0=gt, scalar=coef, in1=go, op0=ALU.mult, op1=ALU.mult))

    def Avec(s, k): return vec[:, (3 * s + 1) * 16 + k:(3 * s + 1) * 16 + k + 1]
    def Bvec(s, k): return vec[:, (3 * s) * 16 + k:(3 * s) * 16 + k + 1]
    def Cvec(s, k): return vec[:, (3 * s + 2) * 16 + k:(3 * s + 2) * 16 + k + 1]
    pr = F(A0 + 808, 128)
    dve.op([c_ready], lambda e: e.tensor_tensor(out=pr[:, 0:64], in0=lamt[:, 0:64], in1=lamt[:, 64:128], op=ALU.mult))
    dve.op([dve.last()], lambda e: e.tensor_tensor(out=pr[:, 64:128], in0=lamt[:, 128:192], in1=lamt[:, 192:256], op=ALU.mult))
    dve.op([dve.last()], lambda e: e.tensor_reduce(out=lamr[:, 0:1], in_=pr[:, 0:64], axis=AX.X, op=ALU.add))
    tl = dve.op([dve.last()], lambda e: e.tensor_reduce(out=lamr[:, 1:2], in_=pr[:, 64:128], axis=AX.X, op=ALU.add))
    tl = act.op([tl], lambda e: e.activation(out=lamr[:, 2:4], in_=lamr[:, 0:2], func=AF.Exp))
    dve.op([tl], lambda e: e.tensor_tensor(out=lamr[:, 4:5], in0=lamr[:, 3:4], in1=lamr[:, 2:3], op=ALU.subtract))
    dve.op([dve.last()], lambda e: e.tensor_scalar(out=nlam, in0=lamr[:, 4:5], scalar1=-0.2, scalar2=None, op0=ALU.add))
    barrier()
    if STOP == 'A':
        es.close(); return nc

    def rstd_from_banks(banks, dst, n_feat):
        for g, bk in enumerate(banks):
            t = act.op([pe.last()], lambda e, g=g, bk=bk: e.activation(out=dst[:, g * TG:(g + 1) * TG], in_=ps[:, bk, 0:TG],
                                                                    func=AF.Ln, scale=1.0 / n_feat, bias=EPS))
            act.op([t], lambda e, g=g: e.activation(out=dst[:, g * TG:(g + 1) * TG], in_=dst[:, g * TG:(g + 1) * TG],
                                                    func=AF.Exp, scale=-0.5))
        return act.last()

    def norm_mod(s, have_stats=False, final_barrier=True):
        ones32 = F(A0, 128)
        sq = [F(A0 + 128 + i * NT, NT) for i in range(2)]
        rstd = F(A0 + 128 + 2 * NT, NT)
        tmp = [F(A0 + 128 + (3 + i) * NT, NT) for i in range(2)]
        htoks = []
        t1 = dve.op([], lambda e: e.memset(ones32, 1.0))
        sqfree = [[], []]
        for kc in (range(KC) if not have_stats else []):
            i = kc % 2
            ts = act.op(sqfree[i], lambda e, i=i, kc=kc: e.activation(out=sq[i], in_=Yv[:, kc, :], func=AF.Square))
            for g in range(NG):
                tp = pe.op([ts, t1], lambda e, i=i, g=g, kc=kc: e.matmul(ps[:, g, 0:TG], lhsT=ones32, rhs=sq[i][:, g * TG:(g + 1) * TG],
                                                                       start=(kc == 0), stop=(kc == KC - 1)))
            sqfree[i] = [tp]
        tr_ = rstd_from_banks(list(range(NG)), rstd, D)
        tfree = [[], []]
        for kc in range(KC):
            i = kc % 2
            td = dve.op([tr_] + tfree[i], lambda e, i=i, kc=kc: e.scalar_tensor_tensor(
                out=tmp[i], in0=Yv[:, kc, :], scalar=Avec(s, kc), in1=rstd, op0=ALU.mult, op1=ALU.mult))
            ta = act.op([td], lambda e, i=i, kc=kc: e.activation(out=Hv[:, kc, :], in_=tmp[i], func=AF.Identity, bias=Bvec(s, kc), scale=1.0))
            tfree[i] = [ta]
            htoks.append(ta)
        if final_barrier:
            barrier()
        return htoks

    def linear_units(ntiles_per_unit, nunits, kn, rhs_fn, evac_fn, nrot, bank0=0, kdeps=None):
        bfree = {}
        pending = []

        def flush(maxpend):
            while len(pending) > maxpend:
                pu, pb, pl = pending.pop(0)
                fr = evac_fn(pu, pb, pl)
                for t in range(ntiles_per_unit):
                    for g in range(NG):
                        bfree[pb[t][g]] = fr[t][g]
        for u in range(nunits):
            flush(nrot - 1)
            r = u % nrot
            bi = bank0 + r * ntiles_per_unit * NG
            banks = [[bi + t * NG + g for g in range(NG)] for t in range(ntiles_per_unit)]
            lasts = []
            for t in range(ntiles_per_unit):
                ktiles = kn[t]
                kbase = 0
                ntl = len(ktiles)
                lp = None
                for ti, kcnt in enumerate(ktiles):
                    wi, wv, wt = w_fetch()
                    for k in range(kcnt):
                        for g in range(NG):
                            lp = pe.op([wt, bfree.get(banks[t][g]), (kdeps[kbase + k] if kdeps else None)], lambda e, t=t, g=g, k=k, kbase=kbase, wv=wv, ti=ti, ntl=ntl, kcnt=kcnt: e.matmul(
                                ps[:, banks[t][g], 0:TG], lhsT=wv[:, k, :], rhs=rhs_fn(t, kbase + k, g),
                                start=(ti == 0 and k == 0), stop=(ti == ntl - 1 and k == kcnt - 1)))
                    w_release(wi, [lp])
                    kbase += kcnt
                lasts.append(lp)
            pending.append((u, banks, lasts))
        flush(0)

    def ntiles_per_unit_banks(n):
        return n * NG

    def ffn(s, last, htoks=None):
        sg = [B(Y0 + i * (TG // 2), TG // 2) for i in range(4)]
        sgfree = [[] for _ in range(4)]
        cnt = [0]

        def evac_g(m, banks, lasts):
            fr = [[None] * NG, [None] * NG]
            for g in range(NG):
                i = cnt[0] % 4; cnt[0] += 1
                ta = act.op([lasts[0]] + sgfree[i], lambda e, i=i, g=g: e.activation(out=sg[i], in_=ps[:, banks[0][g], 0:TG], func=AF.Silu))
                td = dve.op([ta, lasts[1]], lambda e, i=i, g=g: e.tensor_tensor(out=ATv[:, m, g * TG:(g + 1) * TG], in0=ps[:, banks[1][g], 0:TG], in1=sg[i], op=ALU.mult))
                sgfree[i] = [td]
                fr[0][g] = ta; fr[1][g] = td
            return fr
        linear_units(2, FC, [[16], [16]], lambda t, k, g: Hv[:, k, g * TG:(g + 1) * TG], evac_g, nrot=(8 // (2 * NG)), kdeps=htoks)
        barrier()
        down_phase(s, [22, 22], lambda t, k, g: ATv[:, k, g * TG:(g + 1) * TG], last)

    def down_phase(s, ktiles, rhs_fn, last, T0=H0):
        ones32 = F(T0, 128)
        sq = [F(T0 + 128 + i * NT, NT) for i in range(2)]
        rstd = F(T0 + 128 + 2 * NT, NT)
        xin = [F(T0 + 128 + (3 + i) * NT, NT) for i in range(4)]
        t1 = dve.op([], lambda e: e.memset(ones32, 1.0))
        xis = [mk("xi%d_%d_%d" % (i, s, last)) for i in range(4)]
        xtok = {}
        for dc_ in range(4):
            xtok[dc_] = xis[dc_].issue(sp, [], xin[dc_], xs[dc_])
        sqfree = [[], []]
        nrot = (8 - NG) // NG
        nrot = min(nrot, 3)
        SB = 8 - NG
        cnt = [0]

        def evac_d(dc, banks, lasts):
            fr = [[None] * NG]
            for g in range(NG):
                dst = Yv[:, dc, g * TG:(g + 1) * TG]
                if (cnt[0] % 2) == 0:
                    fr[0][g] = act.op([lasts[0]], lambda e, g=g, dst=dst: e.activation(out=dst, in_=ps[:, banks[0][g], 0:TG], func=AF.Copy))
                else:
                    fr[0][g] = dve.op([lasts[0]], lambda e, g=g, dst=dst: e.tensor_copy(out=dst, in_=ps[:, banks[0][g], 0:TG]))
                cnt[0] += 1
            i = dc % 2
            ts = act.op(fr[0] + sqfree[i], lambda e, i=i: e.activation(out=sq[i], in_=Yv[:, dc, :], func=AF.Square))
            for g in range(NG):
                tp = pe.op([ts, t1], lambda e, i=i, g=g: e.matmul(ps[:, SB + g, 0:TG], lhsT=ones32, rhs=sq[i][:, g * TG:(g + 1) * TG],
                                                                 start=(dc == 0), stop=(dc == 15)))
            sqfree[i] = [tp]
            return fr
        linear_units(1, 16, [ktiles], rhs_fn, evac_d, nrot=nrot)
        tr_ = rstd_from_banks([SB + g for g in range(NG)], rstd, D)
        xsv = xs
        for dc in range(16):
            i = dc % 2
            i4 = dc % 4
            lt = xtok.pop(dc)
            td = dve.op([tr_], lambda e, dc=dc: e.scalar_tensor_tensor(out=Yv[:, dc, :], in0=Yv[:, dc, :], scalar=Cvec(s, dc), in1=rstd,
                                                                      op0=ALU.mult, op1=ALU.mult))
            td = dve.op([td, lt], lambda e, dc=dc, i4=i4: e.tensor_tensor(out=Yv[:, dc, :], in0=Yv[:, dc, :], in1=xin[i4], op=ALU.add))
            if dc + 4 < 16:
                xtok[dc + 4] = xis[i4].issue(sp, [td], xin[i4], xsv[dc + 4])
            if not last:
                xo.issue(sp, [td, lt], xsv[dc], Yv[:, dc, :])
                ts2 = act.op([td] + sqfree[i], lambda e, i=i, dc=dc: e.activation(out=sq[i], in_=Yv[:, dc, :], func=AF.Square))
                for g in range(NG):
                    tp2 = pe.op([ts2, t1], lambda e, i=i, g=g, dc=dc: e.matmul(ps[:, g, 0:TG], lhsT=ones32, rhs=sq[i][:, g * TG:(g + 1) * TG],
                                                                       start=(dc == 0), stop=(dc == 15)))
                sqfree[i] = [tp2]
        barrier()

    ht0 = norm_mod(0, final_barrier=False)
    if STOP == 'N1':
        es.close(); return nc
    ffn(0, False, ht0)
    if STOP == 'F1':
        es.close(); return nc
    norm_mod(1, have_stats=True)

    QTv = B(A0, 8 * NT).rearrange("p (k t) -> p k t", t=NT)
    VG0 = A0 + 8 * NT
    VGv = F(VG0, 8 * NP * 288).rearrange("p (j q i) -> p j q i", q=NP, i=288)
    o2 = VG0 + 8 * NP * 288
    KOWN = B(o2, NT).rearrange("p (m t) -> p m t", t=NT); o2 += NT
    VOWN = B(o2, NB * 65).rearrange("p (b e) -> p b e", e=130); o2 += NB * 65
    PT = [B(o2 + i * 256, 256) for i in range(4)]; o2 += 1024
    DTAB = B(o2, 512).rearrange("p (a b) -> p a b", b=512); o2 += 512
    BIASH = F(o2, 160); o2 += 160
    DIST = F(o2, 160); o2 += 160
    HWT = F(o2, 16); o2 += 16
    O1 = [F(o2 + i * 128, 128) for i in range(2)]; o2 += 256
    OT = [F(o2 + i * 128, 128) for i in range(2)]; o2 += 256
    YAT = [B(o2 + i * 64, 64) for i in range(2)]; o2 += 128
    GSUB = F(o2, 128); o2 += 128
    IDB = B(o2, 64); o2 += 64
    ZB = B(o2, 256); o2 += 256
    RC = F(o2, 16); o2 += 16
    JNK = F(o2, 128); o2 += 128
    assert o2 <= A0 + 22528, o2
    CACC = F(Y0, 8 * NT).rearrange("p (j q i) -> p j q i", q=NP, i=256)
    CACCf = F(Y0, 8 * NT).rearrange("p (j t) -> p j t", t=NT)
    KV0 = Y0 + 8192
    KSL = [B(KV0 + i * 3088, 2048).rearrange("p (m t) -> p m t", t=2048) for i in range(2)]
    VSL = [B(KV0 + i * 3088 + 2048, 1040).rearrange("p (b e) -> p b e", e=130) for i in range(2)]
    SGM = [F(Y0 + i * TG, TG) for i in range(2)]
    QTMP = [B(Y0 + 1024 + i * (NT // 2), NT // 2) for i in range(2)]
    KTMP = [B(Y0 + 2048 + i * (NT // 2), NT // 2) for i in range(2)]
    VTT = [B(Y0 + 3072 + i * (NT // 2), NT // 2) for i in range(2)]
    VTOK = [B(Y0 + 4096 + i * (NB * 64), NB * 64).rearrange("p (b e) -> p b e", e=128) for i in range(2)]
    TAILR = [F(KV0 + i * 1024, 8 * NP * 32).rearrange("p (j q i) -> p j q i", q=NP, i=32) for i in range(2)]
    WROW = F(KV0 + 2048, 1024)
    IDF2 = F(KV0 + 3072, 128)
    WDW = F(Y0 + 15392, 248).rearrange("p (j k) -> p j k", k=31)

    mc = mk("mc")
    mc.issue(sp, [], IDB, identb_d)
    mc.issue(sp, [], QTv[64:66, :, :], qaug_d)
    mc.issue(sp, [], DIST[:, 0:NE], ctab_d[:, 16:16 + NE])
    mc.issue(sp, [], HWT, ctab_d[:, 0:16])
    mc.issue(sp, [], GSUB, subg.broadcast_to([128, 128]))
    mc.issue(sp, [], WROW[0:31, :], w_dw)
    mc.issue(sp, [], IDF2, ident_d)
    mready = mc.last()
    dve.op([mready], lambda e: e.tensor_scalar(out=GSUB, in0=GSUB, scalar1=0.8, scalar2=None, op0=ALU.mult))
    dve.op([], lambda e: e.memset(ZB, 0.0))
    for j in range(8):
        tp = pe.op([mready], lambda e, j=j: e.transpose(out=ps[:, 7, j * 32:j * 32 + 31], in_=WROW[0:31, j * 128:(j + 1) * 128], identity=IDF2[0:31, 0:31]))
    dve.op([tp], lambda e: e.tensor_copy(out=WDW, in_=ps[:, 7, 0:256].rearrange("p (j k) -> p j k", k=32)[:, :, 0:31]))
    barrier()
    sgfree = [[], []]
    cnt = [0]

    def evac_a(j, banks, lasts):
        fr = [[None] * NG, [None] * NG]
        for g in range(NG):
            i = cnt[0] % 2; cnt[0] += 1
            ta = act.op([lasts[1]] + sgfree[i], lambda e, i=i, g=g: e.activation(out=SGM[i], in_=ps[:, banks[1][g], 0:TG], func=AF.Sigmoid,
                                                                               bias=vec[:, 248 + j:249 + j], scale=1.0))
            npg = TG // 256
            dst = VGv[:, j, g * npg:(g + 1) * npg, 32:288]
            td = dve.op([ta, lasts[0]], lambda e, i=i, g=g, dst=dst, npg=npg: e.scalar_tensor_tensor(
                out=dst, in0=ps[:, banks[0][g], 0:TG].rearrange("p (q i) -> p q i", i=256), scalar=vec[:, 240 + j:241 + j],
                in1=SGM[i].rearrange("p (q i) -> p q i", i=256), op0=ALU.add, op1=ALU.mult))
            sgfree[i] = [td]
            fr[0][g] = td; fr[1][g] = ta
        return fr
    linear_units(2, 8, [[16], [16]], lambda t, k, g: Hv[:, k, g * TG:(g + 1) * TG], evac_a, nrot=(8 // (2 * NG)))
    tiv = tails_in.ap().rearrange("(j c) (q i) -> c j q i", c=128, i=32)
    for j_ in range(8):
        tl_o = sd.issue(sp, [dve.last()], tiv[:, j_], VGv[:, j_, :, 256:288])
    t_agt = allgather([tl_o], tails_in, tails_all)
    barrier()
    if STOP == 'M1a':
        es.close(); return nc
    qd = mk("qd"); kd = mk("kd"); vd = mk("vd")
    qfree = [[], []]; kfree = [[], []]; vtfree = [[], []]; vkfree = [[], []]
    vtb_free = [None]
    t_agk = [None] * 4; t_agv = [None] * 4

    def evac_qkv(h, banks, lasts):
        i = h % 2
        fr = [[None] * NG for _ in range(3)]
        for g in range(NG):
            fr[0][g] = act.op([lasts[0]] + qfree[i], lambda e, g=g: e.activation(out=QTMP[i][:, g * TG:(g + 1) * TG], in_=ps[:, banks[0][g], 0:TG], func=AF.Copy, scale=0.125))
            fr[1][g] = dve.op([lasts[1]] + kfree[i], lambda e, g=g: e.tensor_copy(out=KTMP[i][:, g * TG:(g + 1) * TG], in_=ps[:, banks[1][g], 0:TG]))
            fr[2][g] = act.op([lasts[2]] + vtfree[i], lambda e, g=g: e.activation(out=VTT[i][:, g * TG:(g + 1) * TG], in_=ps[:, banks[2][g], 0:TG], func=AF.Copy))
        tq0 = qd.issue(sp, fr[0], QTv[0:64, 2 * h, :], QTMP[i][0:64, :])
        tq1 = qd.issue(sp, fr[0], QTv[0:64, 2 * h + 1, :], QTMP[i][64:128, :])
        qfree[i] = [tq1]
        tk = kd.issue(sp, fr[1], k_in[h // 2][(h % 2) * 128:(h % 2 + 1) * 128, :], KTMP[i][:, :])
        kfree[i] = [tk]
        psb = ps[:, 6, :].bitcast(BF16)
        for tb in range(NB):
            tp = pe.op(fr[2] + [vtb_free[0], mready], lambda e, tb=tb: e.transpose(out=psb[:, tb * 128:(tb + 1) * 128], in_=VTT[i][:, tb * 128:(tb + 1) * 128], identity=IDB))
        vtfree[i] = [tp]
        tc = dve.op([tp] + vkfree[i], lambda e: e.tensor_copy(out=VTOK[i], in_=psb[:, 0:NB * 128].rearrange("p (b e) -> p b e", e=128)))
        vtb_free[0] = tc
        tv_ = vd.issue(sp, [tc], v_in[h // 2][(h % 2) * NT:(h % 2 + 1) * NT, :].rearrange("(b p) e -> p b e", p=128), VTOK[i])
        vkfree[i] = [tv_]
        if h % 2 == 1:
            t_agk[h // 2] = allgather([kd.last()], k_in[h // 2], k_all[h // 2])
            t_agv[h // 2] = allgather([vd.last()], v_in[h // 2], v_all[h // 2])
        return fr
    nrot_q = 1 if NG == 2 else 2
    linear_units(3, 8, [[16], [16], [16]], lambda t, k, g: Hv[:, k, g * TG:(g + 1) * TG], evac_qkv, nrot=nrot_q)
    barrier()
    YCv = Hv
    tav = tails_all.ap().rearrange("(r j c) (q i) -> r c j q i", r=8, c=128, i=32)
    tsem = [mk("tl0"), mk("tl1")]; tfree = [[], []]
    first = True
    for r in range(9):
        i = r % 2
        lt = tsem[i].issue(sp, [t_agt] + tfree[i], TAILR[i], tav[min(r, 7)])
        if r < 8:
            if first:
                td = dve.op([lt], lambda e, i=i, r=r: e.tensor_scalar(out=VGv[:, :, :, 0:32], in0=TAILR[i], scalar1=HWT[:, r:r + 1], scalar2=None, op0=ALU.mult))
                first = False
            else:
                td = dve.op([lt, td], lambda e, i=i, r=r: e.scalar_tensor_tensor(out=VGv[:, :, :, 0:32], in0=TAILR[i], scalar=HWT[:, r:r + 1],
                                                                              in1=VGv[:, :, :, 0:32], op0=ALU.mult, op1=ALU.add))
        else:
            for q_ in range(1, NP):
                td = dve.op([lt, td], lambda e, i=i, q_=q_: e.scalar_tensor_tensor(out=VGv[:, :, q_, 0:32], in0=TAILR[i][:, :, q_ - 1, :], scalar=HWT[:, 8:9],
                                                                               in1=VGv[:, :, q_, 0:32], op0=ALU.mult, op1=ALU.add))
        tfree[i] = [td]
    halo_ready = td
    barrier()
    if STOP == 'HALO':
        es.close(); return nc
    conv_ops = []
    conv_last = {}
    for k in range(31):
        for j in range(8):
            if k == 0:
                def f0(j=j):
                    conv_last[j] = dve.op([halo_ready], lambda e: e.tensor_scalar(
                        out=CACC[:, j], in0=VGv[:, j, :, 2:258], scalar1=WDW[:, j, 0:1], scalar2=vec[:, 256 + j:257 + j], op0=ALU.mult, op1=ALU.add))
                conv_ops.append(f0)
            else:
                def fk(j=j, k=k):
                    conv_last[j] = dve.op([conv_last[j]], lambda e: e.scalar_tensor_tensor(
                        out=CACC[:, j], in0=VGv[:, j, :, 2 + k:258 + k], scalar=WDW[:, j, k:k + 1], in1=CACC[:, j], op0=ALU.mult, op1=ALU.add))
                conv_ops.append(fk)

    def emit_conv(n):
        for _ in range(n):
            if not conv_ops:
                return
            f = conv_ops.pop(0)
            f()

    kvc = mk("kvc")
    for i in range(2):
        kvc.issue(sp, [], KSL[i][64:66, :, :], kaug_d)
        dve.op([], lambda e, i=i: e.memset(VSL[i][:, :, 128:130], 1.0))
    dve.op([], lambda e: e.memset(VOWN[:, :, 128:130], 1.0))
    dve.op([], lambda e: e.memset(KOWN[64:66, :, :], 0.0))
    kv_const = [kvc.last(), dve.last()]
    kvs = [mk("kv0"), mk("kv1")]; kvfree = [[], []]
    hs = mk("hs")
    kav = [k_all[i].ap().rearrange("(r m d) t -> d m r t", r=8, d=64) for i in range(4)]
    vav = [v_all[i].ap().rearrange("(r h t) e -> r h t e", r=8, h=2) for i in range(4)]
    slopes = [2.0 ** (-(h + 1)) for h in range(NH)]
    chunk_seq = [(h, p, ch) for h in range(NH) for p in range(NP) for ch in range(p + 1)]
    cst_ = {"i": 0}
    ctoks = {}

    def kv_issue(ci):
        h, p, ch = chunk_seq[ci]; s = ci % 2
        toks_ = []
        for m in range(2):
            src = kav[h // 2][:, 2 * (h % 2) + m, :, ch * 256:(ch + 1) * 256]
            dst = KSL[s][0:64, m, :].rearrange("p (r t) -> p r t", t=256)
            toks_.append(kvs[s].issue(sp, kvfree[s] + [t_agk[h // 2], t_agv[h // 2]], dst, src))
        for b_ in range(2):
            srcv = vav[h // 2][:, h % 2, ch * 256 + b_ * 128:ch * 256 + (b_ + 1) * 128, :].rearrange("r p e -> p r e")
            dstv = VSL[s][:, :, 0:128].rearrange("p (r b) e -> p r b e", b=2)[:, :, b_, :]
            toks_.append(kvs[s].issue(sp, [], dstv, srcv))
        return [toks_[-1]]

    sfree = {0: None, 1: None}
    ofree = {}
    ptfree = [[] for _ in range(4)]
    steps = []
    for h in range(NH):
        e0 = 0
        for p in range(NP):
            npast = 16 * p + 14
            for kb in range(npast):
                steps.append(("past", h, p, kb, e0 + kb))
            e0 += npast
            for kd_ in range(2):
                steps.append(("diag", h, p, kd_, None))
    nsteps = len(steps)
    state = {"ci": -1, "cur": None, "head": -1, "ktoks": None, "htoks": None}
    hfree = [[]]
    unit_idx = {}
    ui = 0
    for h in range(NH):
        for p in range(NP):
            unit_idx[(h, p)] = ui; ui += 1

    def head_setup(h):
        deps = hfree[0]
        a_ = hs.issue(sp, deps, KOWN[0:64, :, :], k_in[h // 2].ap()[(h % 2) * 128:(h % 2 + 1) * 128, :].rearrange("(m d) t -> d m t", d=64))
        b_ = hs.issue(sp, deps, VOWN[:, :, 0:128], v_in[h // 2].ap()[(h % 2) * NT:(h % 2 + 1) * NT, :].rearrange("(b p) e -> p b e", p=128))
        c_ = hs.issue(sp, deps, DTAB, dtab_d[h])
        tb_ = dve.op(deps + [mready], lambda e: e.tensor_scalar(out=BIASH[:, 0:NE], in0=DIST[:, 0:NE], scalar1=-slopes[h], scalar2=None, op0=ALU.mult))
        return {"k": [c_, kd.last()] + kv_const, "v": [c_, vd.last()] + kv_const, "bias": [tb_]}

    qk_tok = {}
    for ci0 in range(min(2, len(chunk_seq))):
        ctoks[ci0] = kv_issue(ci0)

    def emit_qk(si):
        kind, h, p, kb, ei = steps[si]
        sb = si % 2
        if h != state["head"]:
            state["htoks"] = head_setup(h); state["head"] = h
        if kind == "past":
            ch = kb // 16; kk = kb % 16
            key = (h, p, ch)
            if state["cur"] != key:
                state["ci"] += 1; ci = state["ci"]
                assert chunk_seq[ci] == key, (chunk_seq[ci], key)
                assert ci in ctoks, ci
                state["cur"] = key; state["ktoks"] = ctoks.pop(ci)
            ci = state["ci"]; s = ci % 2
            deps = state["ktoks"] + [sfree[sb], qd.last(), mready] + kv_const
            for m in range(2):
                t = pe.op(deps, lambda e, m=m, s=s, kk=kk: e.matmul(ps[:, sb, m * 256:(m + 1) * 256], lhsT=KSL[s][0:66, m, kk * 128:(kk + 1) * 128],
                                                                    rhs=QTv[0:66, 2 * h + m, p * 256:(p + 1) * 256], start=True, stop=True))
            qk_tok[si] = (t, s, kk, ci)
        else:
            deps = state["htoks"]["k"] + [sfree[sb], qd.last(), mready]
            pe.op(deps, lambda e: e.matmul(ps[:, sb, 0:512], lhsT=IDB, rhs=DTAB[:, kb, :], start=True, stop=False))
            for m in range(2):
                t = pe.op([], lambda e, m=m: e.matmul(ps[:, sb, m * 256:(m + 1) * 256], lhsT=KOWN[0:66, m, p * 256 + kb * 128:p * 256 + (kb + 1) * 128],
                                                     rhs=QTv[0:66, 2 * h + m, p * 256:(p + 1) * 256], start=False, stop=(m == 1)))
            qk_tok[si] = (t, None, None, None)

    ps_t = ps[:, 7, :].bitcast(BF16)
    tfree_b = [None]
    epi_cnt = [0]
    deferred = []
    yat_free = [None, None]

    def run_deferred(force=False):
        for d_ in list(deferred):
            d_[0] -= 1
            if force or d_[0] <= 0:
                deferred.remove(d_); d_[1]()

    def epilogue(h, p, ob, lastpv):
        fr = []
        for qb in range(2):
            bk = ob + qb
            i = epi_cnt[0] % 2; epi_cnt[0] += 1
            r0 = RC[:, i * 8:i * 8 + 8]
            dve.op([lastpv], lambda e: e.reciprocal(out=r0[:, 0:1], in_=ps[:, bk, 128:129]))
            dve.op([lastpv], lambda e: e.reciprocal(out=r0[:, 1:2], in_=ps[:, bk, 257:258]))
            dve.op([dve.last()], lambda e: e.tensor_tensor(out=r0[:, 2:3], in0=r0[:, 1:2], in1=nlam, op=ALU.mult))
            dve.op([dve.last()], lambda e: e.tensor_scalar(out=O1[i], in0=ps[:, bk, 0:128], scalar1=r0[:, 0:1], scalar2=None, op0=ALU.mult))
            t_o = dve.op([dve.last()], lambda e: e.scalar_tensor_tensor(out=OT[i], in0=ps[:, bk, 129:257], scalar=r0[:, 2:3], in1=O1[i], op0=ALU.mult, op1=ALU.add))
            fr.append(t_o)
            dve.op([t_o], lambda e: e.tensor_tensor(out=JNK, in0=OT[i], in1=OT[i], op=ALU.mult))
            t_s = dve.op([dve.last()], lambda e: e.tensor_reduce(out=r0[:, 3:4], in_=JNK, axis=AX.X, op=ALU.add))
            t_l = act.op([t_s], lambda e: e.activation(out=r0[:, 4:5], in_=r0[:, 3:4], func=AF.Ln, scale=1.0 / 128, bias=EPS))
            t_l = act.op([t_l], lambda e: e.activation(out=r0[:, 5:6], in_=r0[:, 4:5], func=AF.Exp, scale=-0.5))
            t_y = dve.op([t_l, yat_free[i]], lambda e: e.scalar_tensor_tensor(out=YAT[i], in0=OT[i], scalar=r0[:, 5:6], in1=GSUB, op0=ALU.mult, op1=ALU.mult))
            def fin(t_y=t_y, i=i, qb=qb):
                t_p = pe.op([t_y, tfree_b[0]], lambda e: e.transpose(out=ps_t[:, 0:128], in_=YAT[i], identity=IDB))
                tfree_b[0] = dve.op([t_p], lambda e: e.tensor_copy(out=YCv[:, 8 + h, p * 256 + qb * 128:p * 256 + (qb + 1) * 128], in_=ps_t[:, 0:128]))
                yat_free[i] = t_p
            deferred.append([8 + 3 * qb, fin])
        return fr

    emit_qk(0)
    cur_unit = None
    nconv_per_unit = (len(conv_ops) + NH * NP - 1) // (NH * NP) + 1
    for si in range(nsteps):
        kind, h, p, kb, ei = steps[si]
        u = unit_idx[(h, p)]; ob = 2 + 2 * (u % 2)
        if cur_unit != u:
            cur_unit = u
            for qb in range(2):
                pe.op([ofree.get(ob + qb), mready], lambda e, qb=qb: e.matmul(ps[:, ob + qb, 0:512], lhsT=ZB[:, 0:128], rhs=ZB[:, 0:512], start=True, stop=False))
        same_head_next = (si + 1 < nsteps and steps[si + 1][1] == h)
        if same_head_next:
            emit_qk(si + 1)
        tq, s, kk, ci = qk_tok.pop(si)
        sb = si % 2
        pi = si % 4
        if kind == "past":
            ta = act.op([tq] + ptfree[pi] + state["htoks"]["bias"], lambda e: e.activation(out=PT[pi], in_=ps[:, sb, :], func=AF.Exp, bias=BIASH[:, ei:ei + 1], scale=1.0))
        else:
            ta = act.op([tq] + ptfree[pi], lambda e: e.activation(out=PT[pi], in_=ps[:, sb, :], func=AF.Exp))
        sfree[sb] = ta
        is_last = (kind == "diag" and kb == 1)
        for qb in range(2):
            for m in range(2):
                if kind == "past":
                    rhs = VSL[s][:, kk, 0:129]
                else:
                    rhs = VOWN[:, 2 * p + kb, 0:129]
                tpv = pe.op([ta] + (state["htoks"]["v"] if kind == "diag" else []), lambda e, qb=qb, m=m, rhs=rhs: e.matmul(ps[:, ob + qb, m * 129:(m + 1) * 129], lhsT=PT[pi][:, m * 256 + qb * 128:m * 256 + (qb + 1) * 128],
                                                                        rhs=rhs, start=False, stop=is_last))
        ptfree[pi] = [tpv]
        run_deferred(force=is_last)
        if kind == "past" and (kk == 15 or kb == 16 * p + 13):
            kvfree[s] = [tpv]
            if ci + 2 < len(chunk_seq):
                ctoks[ci + 2] = kv_issue(ci + 2)
        if is_last:
            fr = epilogue(h, p, ob, tpv)
            ofree[ob] = fr[0]; ofree[ob + 1] = fr[1]
            emit_conv(nconv_per_unit)
            if p == NP - 1:
                hfree[0] = [tpv]
        if (not same_head_next) and si + 1 < nsteps:
            emit_qk(si + 1)
    run_deferred(force=True)
    emit_conv(len(conv_ops))
    barrier()
    if STOP == 'ATT':
        es.close(); return nc
    LT0 = VG0
    ones32 = F(LT0, 128)
    sqc = [F(LT0 + 128 + i * NT, NT) for i in range(2)]
    mean = F(LT0 + 128 + 2 * NT, NT); rs_ = F(LT0 + 128 + 3 * NT, NT); tmpc = [F(LT0 + 128 + (4 + i) * NT, NT) for i in range(2)]
    t1 = dve.op([], lambda e: e.memset(ones32, 1.0))
    sqfree = [[], []]
    for j in range(8):
        i = j % 2
        ts = act.op(sqfree[i], lambda e, i=i, j=j: e.activation(out=sqc[i], in_=CACCf[:, j, :], func=AF.Square))
        for g in range(NG):
            pe.op([t1], lambda e, g=g, j=j: e.matmul(ps[:, g, 0:TG], lhsT=ones32, rhs=CACCf[:, j, g * TG:(g + 1) * TG], start=(j == 0), stop=(j == 7)))
            tp = pe.op([ts], lambda e, g=g, i=i, j=j: e.matmul(ps[:, NG + g, 0:TG], lhsT=ones32, rhs=sqc[i][:, g * TG:(g + 1) * TG], start=(j == 0), stop=(j == 7)))
        sqfree[i] = [tp]
    for g in range(NG):
        sl = slice(g * TG, (g + 1) * TG)
        dve.op([pe.last()], lambda e, g=g, sl=sl: e.tensor_scalar(out=mean[:, sl], in0=ps[:, g, 0:TG], scalar1=1.0 / 1024, scalar2=None, op0=ALU.mult))
        dve.op([dve.last()], lambda e, sl=sl: e.tensor_tensor(out=rs_[:, sl], in0=mean[:, sl], in1=mean[:, sl], op=ALU.mult))
        dve.op([dve.last()], lambda e, g=g, sl=sl: e.scalar_tensor_tensor(out=rs_[:, sl], in0=ps[:, NG + g, 0:TG], scalar=1.0 / 1024, in1=rs_[:, sl], op0=ALU.mult, op1=ALU.subtract))
        ta = act.op([dve.last()], lambda e, sl=sl: e.activation(out=rs_[:, sl], in_=rs_[:, sl], func=AF.Ln, bias=EPS, scale=1.0))
        act.op([ta], lambda e, sl=sl: e.activation(out=rs_[:, sl], in_=rs_[:, sl], func=AF.Exp, scale=-0.5))
    tr_ = act.last()
    tcf = [[], []]
    for j in range(8):
        i = j % 2
        dve.op([tr_] + tcf[i], lambda e, i=i, j=j: e.tensor_tensor(out=tmpc[i], in0=CACCf[:, j, :], in1=mean, op=ALU.subtract))
        td = dve.op([dve.last()], lambda e, i=i: e.tensor_tensor(out=tmpc[i], in0=tmpc[i], in1=rs_, op=ALU.mult))
        ta = act.op([td], lambda e, i=i, j=j: e.activation(out=YCv[:, j, :], in_=tmpc[i], func=AF.Silu, scale=vec[:, 264 + j:265 + j], bias=vec[:, 272 + j:273 + j]))
        tcf[i] = [ta]
    barrier()
    down_phase(1, [16], lambda t, k, g: YCv[:, k, g * TG:(g + 1) * TG], False, T0=A0)
    if STOP == 'WOUT':
        es.close(); return nc
    ht2 = norm_mod(2, have_stats=True, final_barrier=False)
    ffn(2, True, ht2)
    idf = F(H0, 128)
    ob_ = [F(H0 + 128 + i * 2048, 2048) for i in range(2)]
    osem = mk("oi"); t_id = osem.issue(sp, [], idf, ident_d)
    od = [mk("od0"), mk("od1")]; obfree = [[], []]
    bfree = {}
    evn = 0
    for tb in range(NB):
        s = tb % 2
        for g4 in range(4):
            bank = g4
            for j in range(4):
                kc = g4 * 4 + j
                lp = pe.op([t_id, bfree.get(bank)], lambda e, bank=bank, j=j, kc=kc: e.transpose(
                    out=ps[:, bank, j * 128:(j + 1) * 128], in_=Yv[:, kc, tb * 128:(tb + 1) * 128], identity=idf))
            dst = ob_[s][:, g4 * 512:(g4 + 1) * 512]
            if evn % 2 == 0:
                bfree[bank] = act.op([lp] + obfree[s], lambda e, bank=bank, dst=dst: e.activation(out=dst, in_=ps[:, bank, :], func=AF.Copy))
            else:
                bfree[bank] = dve.op([lp] + obfree[s], lambda e, bank=bank, dst=dst: e.tensor_copy(out=dst, in_=ps[:, bank, :]))
            evn += 1
        to = od[s].issue(sp, [act.last(), dve.last()], out_sh[tb * 128:(tb + 1) * 128, :], ob_[s])
        obfree[s] = [to]
    barrier()
    es.close()
    return nc


def _consts(NP):
    NT = 256 * NP
    NE = sum(16 * p + 14 for p in range(NP))
    bf = ml_dtypes.bfloat16
    slopes = [2.0 ** (-(h + 1)) for h in range(NH)]
    ident = np.eye(128, dtype=np.float32)
    kaug = np.zeros((2, 2, 2048), np.float32)
    kaug[0] = (np.arange(2048) % 128)[None, :]
    kaug[1] = 1.0
    qaug = np.zeros((2, 16, NT), np.float32)
    for h in range(NH):
        for m in range(2):
            qaug[0, 2 * h + m] = slopes[h]
            qaug[1, 2 * h + m] = -slopes[h] * (np.arange(NT) % 256)
    dtab = np.zeros((8, 128, 2, 512), np.float32)
    kk = np.arange(128)[:, None]; qq = np.arange(256)[None, :]
    for h in range(NH):
        for kb in range(2):
            kl = kb * 128 + kk
            allowed = (kl // 64) <= (qq // 64)
            val = np.where(allowed, -slopes[h] * np.abs(qq - kl), -30000.0)
            dtab[h, :, kb, 0:256] = val; dtab[h, :, kb, 256:512] = val
    ctabs = []
    for c in range(NCORE):
        t = np.zeros((128, 16 + NE), np.float32)
        if c >= 1:
            t[:, c - 1] = 1.0
        else:
            t[:, 8] = 1.0
        e = 0
        for p in range(NP):
            j = 8 * p + c
            for kb in range(16 * p + 14):
                t[:, 16 + e] = (256 * j - 128 * kb) if kb < 2 * j else BIG
                e += 1
        ctabs.append(t)
    return dict(ident=ident, identb=ident.astype(bf), kaug=kaug.astype(bf), qaug=qaug.astype(bf), dtab=dtab.astype(bf)), ctabs


_NC_CACHE = {}


def run(inputs, NP):
    NT = 256 * NP
    f = lambda a: np.ascontiguousarray(np.asarray(a, dtype=np.float32))
    x = f(inputs["x"])[0]
    S = x.shape[0]
    assert S == NCORE * NT
    consts, ctabs = _consts(NP)
    shared = dict(
        c=f(inputs["c"]), g_pre=f(inputs["g_pre"]).reshape(48, 128), g_post=f(inputs["g_post"]).reshape(48, 128),
        w1g=f(inputs["w_ffn1_gate"])[0], w1u=f(inputs["w_ffn1_up"])[0], w1d=f(inputs["w_ffn1_down"])[0],
        w3g=f(inputs["w_ffn2_gate"])[0], w3u=f(inputs["w_ffn2_up"])[0], w3d=f(inputs["w_ffn2_down"])[0],
        w_in=f(inputs["w_in"])[0], w_out=f(inputs["w_out"])[0],
        b_inc=f(inputs["b_in_conv"]).reshape(16, 128), w_dw=f(inputs["w_dw"])[0], b_dw=f(inputs["b_dw"]).reshape(8, 128),
        ln_g=f(inputs["conv_ln_g"]).reshape(8, 128), ln_b=f(inputs["conv_ln_b"]).reshape(8, 128),
        lamv=np.concatenate([f(inputs["lam_q1"]), f(inputs["lam_k1"]), f(inputs["lam_q2"]), f(inputs["lam_k2"])], 0),
        subg=f(inputs["subln_g"]).reshape(1, 128), **consts)
    wada = f(inputs["w_ada"])[0]; bada = f(inputs["b_ada"]).reshape(1, -1)
    in_maps = []
    for c in range(NCORE):
        xs_ = np.concatenate([x[(8 * p + c) * 256:(8 * p + c + 1) * 256] for p in range(NP)], 0)
        m = dict(shared)
        m.update(x_sh=np.ascontiguousarray(xs_), wada=np.ascontiguousarray(wada[:, c * 2304:(c + 1) * 2304]),
                 bada=np.ascontiguousarray(bada[:, c * 2304:(c + 1) * 2304]), ctab=ctabs[c])
        in_maps.append(m)
    if NP not in _NC_CACHE:
        _NC_CACHE[NP] = build(NP)
    res = run_bass_kernel_spmd(_NC_CACHE[NP], in_maps, core_ids=list(range(NCORE)))
    out = np.zeros((1, S, D), np.float32)
    for c in range(NCORE):
        o = res.results[c]["out_sh"]
        for p in range(NP):
            out[0, (8 * p + c) * 256:(8 * p + c + 1) * 256] = o[p * 256:(p + 1) * 256]
    return out


def kernel(**inputs):
    return run(inputs, 4)
```
